# Optimizing a Trainium2 kernel written in Bass

```python
import math
import jax, jax.numpy as jnp
from jax import lax
import numpy as np

D_MODEL = 2048
BATCH = 1
SEQ = 16384
DEPTH = 2

SB_HEADS = 8
SB_HEAD_DIM = D_MODEL // 32
RET_HEADS = 8
RET_HEAD_DIM = D_MODEL // 16
SGU_GROUPS = 4
SGU_GROUP_DIM = D_MODEL // 16
SB_W = SB_HEADS * SB_HEAD_DIM
RET_W = RET_HEADS * RET_HEAD_DIM
SGU_W = SGU_GROUPS * SGU_GROUP_DIM
MIX_W = SB_W + RET_W + SGU_W
IN_COLS = 3 * SB_W + 4 * RET_W + 2 * SGU_W

SB_BLOCK = 128
RET_CHUNK = 128
RET_DECAY_BASE = 5.0
ROPE_BASE = 10000.0
SGU_CHUNK = 128

PEER_HEADS = 8
PEER_NKEYS = 128
PEER_EXPERTS = PEER_NKEYS * PEER_NKEYS
PEER_TOPK = 16
PEER_QDIM = 256
PEER_SUBDIM = PEER_QDIM // 2
PEER_TOKEN_CHUNK = 128

DN_ALPHA = (2.0 * DEPTH) ** 0.25
DN_BETA = (8.0 * DEPTH) ** -0.25
LN_EPS = 1e-5

kernel_name = 'hymba_style_sb_ret_sgu_peer_deepnorm'


def _layernorm(x, g, b):
    xf = x.astype(jnp.float32)
    mu = jnp.mean(xf, axis=-1, keepdims=True)
    var = jnp.mean(jnp.square(xf - mu), axis=-1, keepdims=True)
    return ((xf - mu) * lax.rsqrt(var + LN_EPS) * g.astype(jnp.float32) + b.astype(jnp.float32)).astype(x.dtype)


def _stick_breaking(q, k, v):
    b_, s_, h_, dh = q.shape
    nb = s_ // SB_BLOCK
    scale = dh ** -0.5
    qb = q.reshape(b_, nb, SB_BLOCK, h_, dh).transpose(1, 0, 2, 3, 4)
    kf = k.astype(jnp.float32)
    vf = v.astype(jnp.float32)
    key_pos = jnp.arange(s_)

    def one_block(args):
        qblk, bi = args
        z = jnp.einsum('bqhd,bkhd->bhqk', qblk.astype(jnp.float32), kf) * scale
        q_pos = bi * SB_BLOCK + jnp.arange(SB_BLOCK)
        mask = key_pos[None, :] < q_pos[:, None]
        log_beta = jax.nn.log_sigmoid(z)
        log_1m = jnp.where(mask, log_beta - z, 0.0)
        between = lax.cumsum(log_1m, axis=3, reverse=True) - log_1m
        a = jnp.where(mask, jnp.exp(log_beta + between), 0.0)
        return jnp.einsum('bhqk,bkhd->bqhd', a, vf)

    o = lax.map(one_block, (qb, jnp.arange(nb)))
    return o.transpose(1, 0, 2, 3, 4).reshape(b_, s_, h_ * dh).astype(q.dtype)


def _rotary(x, pos):
    half = x.shape[-1] // 2
    inv = ROPE_BASE ** (-jnp.arange(half, dtype=jnp.float32) / half)
    ang = pos.astype(jnp.float32)[:, None] * inv[None, :]
    cos = jnp.cos(ang)[None, :, None, :]
    sin = jnp.sin(ang)[None, :, None, :]
    x1 = x[..., :half].astype(jnp.float32)
    x2 = x[..., half:].astype(jnp.float32)
    return jnp.concatenate([x1 * cos - x2 * sin, x1 * sin + x2 * cos], axis=-1)


def _retention(q, k, v, g):
    b_, s_, h_, d = q.shape
    c_ = RET_CHUNK
    n = s_ // c_
    pos = jnp.arange(s_)
    qf = _rotary(q, pos)
    kf = _rotary(k, pos) * (d ** -0.5)
    vf = v.astype(jnp.float32)
    log_gamma = jnp.log1p(-jnp.exp2(-RET_DECAY_BASE - jnp.arange(h_, dtype=jnp.float32)))
    idx = jnp.arange(c_, dtype=jnp.float32)
    diff = idx[:, None] - idx[None, :]
    decay = jnp.where(diff >= 0, jnp.exp(log_gamma[:, None, None] * jnp.maximum(diff, 0.0)), 0.0)
    qc = qf.reshape(b_, n, c_, h_, d)
    kc = kf.reshape(b_, n, c_, h_, d)
    vc = vf.reshape(b_, n, c_, h_, d)
    scores = jnp.einsum('bnchd,bnshd->bnhcs', qc, kc) * decay
    y_inner = jnp.einsum('bnhcs,bnshe->bnche', scores, vc)
    zeta = jnp.exp(log_gamma[:, None] * (c_ - 1.0 - idx)[None, :])
    kv = jnp.einsum('bnshd,bnshe,hs->nbhde', kc, vc, zeta)
    chunk_decay = jnp.exp(log_gamma * c_)[None, :, None, None]

    def step(state, kv_i):
        return chunk_decay * state + kv_i, state

    _, prev = lax.scan(step, jnp.zeros((b_, h_, d, d), jnp.float32), kv)
    xi = jnp.exp(log_gamma[:, None] * (idx + 1.0)[None, :])
    y_cross = jnp.einsum('bnchd,nbhde,hc->bnche', qc, prev, xi)
    y = y_inner + y_cross
    mu = jnp.mean(y, axis=-1, keepdims=True)
    var = jnp.mean(jnp.square(y - mu), axis=-1, keepdims=True)
    y = ((y - mu) * lax.rsqrt(var + LN_EPS)).reshape(b_, s_, h_ * d)
    return (jax.nn.silu(g.astype(jnp.float32).reshape(b_, s_, h_ * d)) * y).astype(v.dtype)


def _sgu(u, v, ln_g, ln_b, w_s, b_s):
    b_, s_, _ = u.shape
    n = s_ // SGU_CHUNK
    gq, c = SGU_GROUPS, SGU_GROUP_DIM
    u = jax.nn.gelu(u)
    v = jax.nn.gelu(v)
    vg = _layernorm(v.reshape(b_, s_, gq, c), ln_g.reshape(gq, c), ln_b.reshape(gq, c))
    vg = vg.reshape(b_, n, SGU_CHUNK, gq, c)
    w = jnp.tril(w_s)
    mixed = jnp.einsum('gts,bnsgc->bntgc', w, vg) + b_s.T[None, None, :, :, None]
    return u * mixed.reshape(b_, s_, gq * c)


def _peer(x, w_q, sub_keys, u_tab, v_tab):
    b_, s_, d = x.shape
    xt = x.reshape((b_ * s_) // PEER_TOKEN_CHUNK, PEER_TOKEN_CHUNK, d)

    def one_chunk(xc):
        t = xc.shape[0]
        q = (xc @ w_q).reshape(t, PEER_HEADS, 2, PEER_SUBDIM).astype(jnp.float32)
        s = jnp.einsum('thpd,pkd->thpk', q, sub_keys.astype(jnp.float32))
        s_top, i_top = lax.top_k(s, PEER_TOPK)
        cand = s_top[:, :, 0, :, None] + s_top[:, :, 1, None, :]
        best, flat = lax.top_k(cand.reshape(t, PEER_HEADS, PEER_TOPK * PEER_TOPK), PEER_TOPK)
        ia = jnp.take_along_axis(i_top[:, :, 0, :], flat // PEER_TOPK, axis=-1)
        ib = jnp.take_along_axis(i_top[:, :, 1, :], flat % PEER_TOPK, axis=-1)
        expert = (ia * PEER_NKEYS + ib).reshape(t, PEER_HEADS * PEER_TOPK)
        gate = jax.nn.softmax(best, axis=-1).reshape(t, PEER_HEADS * PEER_TOPK)
        u_sel = u_tab[expert]
        hid = jax.nn.gelu(jnp.einsum('ted,td->te', u_sel, xc).astype(jnp.float32))
        coef = (gate * hid).astype(x.dtype)
        return jnp.einsum('te,ted->td', coef, v_tab[expert])

    return lax.map(one_chunk, xt).reshape(b_, s_, d)


def setup_inputs(seed: int = 0) -> dict:
    key = jax.random.key(seed)
    ks = jax.random.split(key, 16)
    L, D = DEPTH, D_MODEL
    f32 = jnp.float32

    def nrm(k, shape, scale):
        return jax.random.normal(k, shape, f32) * scale

    return {
        'x': nrm(ks[0], (BATCH, SEQ, D), 1.0),
        'w_in': nrm(ks[1], (L, D, IN_COLS), D ** -0.5),
        'w_out': nrm(ks[2], (L, MIX_W, D), DN_BETA * MIX_W ** -0.5),
        'sgu_ln_g': 1.0 + nrm(ks[3], (L, SGU_W), 0.02),
        'sgu_ln_b': nrm(ks[4], (L, SGU_W), 0.02),
        'sgu_w': nrm(ks[5], (L, SGU_GROUPS, SGU_CHUNK, SGU_CHUNK), SGU_CHUNK ** -0.5),
        'sgu_b': 1.0 + nrm(ks[6], (L, SGU_GROUPS, SGU_CHUNK), 0.02),
        'ln1_g': 1.0 + nrm(ks[7], (L, D), 0.02),
        'ln1_b': nrm(ks[8], (L, D), 0.02),
        'peer_wq': nrm(ks[9], (L, D, PEER_HEADS * PEER_QDIM), D ** -0.5),
        'peer_sub_keys': nrm(ks[10], (L, 2, PEER_NKEYS, PEER_SUBDIM), PEER_SUBDIM ** -0.5),
        'peer_u': nrm(ks[11], (L, PEER_EXPERTS, D), D ** -0.5),
        'peer_v': nrm(ks[12], (L, PEER_EXPERTS, D), DN_BETA * PEER_HEADS ** -0.5),
        'ln2_g': 1.0 + nrm(ks[13], (L, D), 0.02),
        'ln2_b': nrm(ks[14], (L, D), 0.02),
    }


def reference(x, w_in, w_out, sgu_ln_g, sgu_ln_b, sgu_w, sgu_b, ln1_g, ln1_b,
              peer_wq, peer_sub_keys, peer_u, peer_v, ln2_g, ln2_b):
    b_, s_, _ = x.shape
    sizes = (SB_W,) * 3 + (RET_W,) * 4 + (SGU_W,) * 2
    points = [sum(sizes[:i + 1]) for i in range(len(sizes) - 1)]
    for l in range(DEPTH):
        h = x @ w_in[l]
        sq, sk, sv, rq, rk, rv, rg, cu, cv = jnp.split(h, points, axis=-1)
        sb_shape = (b_, s_, SB_HEADS, SB_HEAD_DIM)
        ret_shape = (b_, s_, RET_HEADS, RET_HEAD_DIM)
        a_out = _stick_breaking(sq.reshape(sb_shape), sk.reshape(sb_shape), sv.reshape(sb_shape))
        r_out = _retention(rq.reshape(ret_shape), rk.reshape(ret_shape), rv.reshape(ret_shape), rg.reshape(ret_shape))
        c_out = _sgu(cu, cv, sgu_ln_g[l], sgu_ln_b[l], sgu_w[l], sgu_b[l])
        mix = jnp.concatenate([a_out, r_out.astype(x.dtype), c_out.astype(x.dtype)], axis=-1) @ w_out[l]
        x = _layernorm(DN_ALPHA * x + mix, ln1_g[l], ln1_b[l])
        ffn = _peer(x, peer_wq[l], peer_sub_keys[l], peer_u[l], peer_v[l])
        x = _layernorm(DN_ALPHA * x + ffn, ln2_g[l], ln2_b[l])
    return x
```

```python
import contextlib
import numpy as np
import ml_dtypes
import concourse.bass as bass
import concourse.mybir as mybir
from concourse.bass_utils import run_bass_kernel_spmd

F32 = mybir.dt.float32
BF16 = mybir.dt.bfloat16
AF = mybir.ActivationFunctionType
ALU = mybir.AluOpType
NPBF = ml_dtypes.bfloat16

D = 2048
SEQ = 16384
DEPTH = 2
NCORE = 8
LN_EPS = 1e-5
DN_ALPHA = (2.0 * DEPTH) ** 0.25


class Buf:
    __slots__ = ("name", "w", "r", "psum")

    def __init__(self, name):
        self.name = name
        self.w = None
        self.r = {}
        self.psum = False


class Tl:
    def __init__(self, t, b):
        self.t = t
        self.b = b

    def __getitem__(self, idx):
        return self.t[idx]


class Sched:
    ENG = ("pe", "act", "dve", "pool", "sp")

    def __init__(self, nc, stack):
        self.nc = nc
        self.stack = stack
        self.ops = {e: [] for e in self.ENG}
        self.esem = {e: stack.enter_context(nc.semaphore("cs_" + e)) for e in self.ENG}
        self.ecnt = {e: 0 for e in self.ENG}
        self.seen = {e: {} for e in self.ENG}
        self.dsem = {}
        self.nbuf = 0
        self.limit = None
        self.nops = 0

    def buf(self, name=None):
        self.nbuf += 1
        return Buf(name or "b%d" % self.nbuf)

    def sb(self, name, shape, dt):
        return Tl(self.stack.enter_context(self.nc.sbuf_tensor(name, shape, dt)), self.buf(name))

    def ps(self, name, shape, dt):
        t = Tl(self.stack.enter_context(self.nc.psum_tensor(name, shape, dt)), self.buf(name))
        t.b.psum = True
        return t

    def op(self, eng, fn, R=(), W=(), dma=None):
        self.nops += 1
        if self.limit is not None and self.nops > self.limit:
            return None
        R = [t.b if isinstance(t, Tl) else t for t in R]
        W = [t.b if isinstance(t, Tl) else t for t in W]
        W = W + [b for b in R if b.psum and b not in W]
        R = [b for b in R if not b.psum]
        deps = []
        for t in R:
            b = t.b if isinstance(t, Tl) else t
            if b.w is not None:
                deps.append(b.w)
        for t in W:
            b = t.b if isinstance(t, Tl) else t
            if b.w is not None:
                deps.append(b.w)
            deps.extend(b.r.values())
        seen = self.seen[eng]
        need = {}
        for (sid, sem, val, src) in deps:
            if eng == "pe" and src == "pe":
                continue
            if seen.get(sid, 0) >= val:
                continue
            if sid not in need or need[sid][1] < val:
                need[sid] = (sem, val)
        waits = []
        for sid, (sem, val) in need.items():
            seen[sid] = val
            waits.append((sem, val))
        if dma is None:
            self.ecnt[eng] += 1
            ev = ("E" + eng, self.esem[eng], self.ecnt[eng], eng)
            inc = (self.esem[eng], 1)
        else:
            if dma not in self.dsem:
                self.dsem[dma] = [self.stack.enter_context(self.nc.semaphore("ds_" + dma)), 0]
            d = self.dsem[dma]
            d[1] += 16
            ev = ("D" + dma, d[0], d[1], "dma")
            inc = (d[0], 16)
        self.ops[eng].append((waits, fn, inc))
        for t in R:
            b = t.b if isinstance(t, Tl) else t
            b.r[ev[0]] = ev
        for t in W:
            b = t.b if isinstance(t, Tl) else t
            b.w = ev
            b.r = {}
        return ev

    def mm(self, out, lhsT, rhs, start, stop, R, W):
        self.op("pe", lambda e: e.matmul(out, lhsT=lhsT, rhs=rhs, start=start, stop=stop), R, W)

    def tr(self, out, in_, ident, R, W):
        self.op("pe", lambda e: e.transpose(out, in_, ident), R, W)

    def act(self, out, in_, func, R, W, bias=None, scale=None, accum=None, eng="act"):
        kw = {}
        if bias is not None:
            kw["bias"] = bias
        if scale is not None:
            kw["scale"] = scale
        if accum is not None:
            kw["accum_out"] = accum
        self.op("act", lambda e: e.activation(out=out, in_=in_, func=func, **kw), R, W)

    def tt(self, eng, out, in0, in1, op, R, W):
        self.op(eng, lambda e: e.tensor_tensor(out=out, in0=in0, in1=in1, op=op), R, W)

    def ts(self, eng, out, in0, s1, op0, R, W, s2=None, op1=None):
        if op1 is None:
            self.op(eng, lambda e: e.tensor_scalar(out=out, in0=in0, scalar1=s1, scalar2=None, op0=op0), R, W)
        else:
            self.op(eng, lambda e: e.tensor_scalar(out=out, in0=in0, scalar1=s1, scalar2=s2, op0=op0, op1=op1), R, W)

    def stt(self, out, in0, scalar, in1, op0, op1, R, W):
        self.op("dve", lambda e: e.scalar_tensor_tensor(out=out, in0=in0, scalar=scalar, in1=in1, op0=op0, op1=op1), R, W)

    def cp(self, eng, out, in_, R, W):
        if eng == "act":
            self.op("act", lambda e: e.activation(out=out, in_=in_, func=AF.Copy), R, W)
        else:
            self.op(eng, lambda e: e.tensor_copy(out=out, in_=in_), R, W)

    def dma(self, eng, out, in_, R, W, tag):
        self.op(eng, lambda e: e.dma_start(out=out, in_=in_), R, W, dma=tag)

    def finish(self, bufs):
        self.op("sp", lambda e: e.nop(), R=list(bufs))

    def emit(self):
        nc = self.nc

        def mk(ops):
            def body(eng):
                for waits, fn, inc in ops:
                    for sem, val in waits:
                        eng.wait_ge(sem, val)
                    ins = fn(eng)
                    ins.then_inc(inc[0], inc[1])
            return body

        with nc.Block() as block:
            block.tensor(mk(self.ops["pe"]))
            block.scalar(mk(self.ops["act"]))
            block.vector(mk(self.ops["dve"]))
            block.gpsimd(mk(self.ops["pool"]))
            block.sync(mk(self.ops["sp"]))


M_NCOL = 1152
M_FM = 640
C_DT, C_XI, C_WM, C_MI, C_ID = 0, 128, 256, 384, 512
C_ZETA, C_CD, C_BS, C_MH, C_LNG, C_LNB, C_W = 640, 641, 642, 643, 644, 708, 776


def build_M(NT=32):
    nc = bass.Bass("TRN2", target_bir_lowering=False)
    SQ = NT * 512
    NCH = NT * 4
    din = lambda n, s, d=F32: nc.dram_tensor(n, s, d, kind="ExternalInput").ap()
    dout = lambda n, s, d=F32: nc.dram_tensor(n, s, d, kind="ExternalOutput").ap()
    xT = din("xT", [NT, 128, 16 * 512])
    w = din("w", [128, 16 * M_NCOL])
    cosT = din("cosT", [128, SQ])
    sinT = din("sinT", [128, SQ])
    cst = din("cst", [128, C_W])
    swT = din("swT", [128, 128])
    sbmask = din("sbmask", [128, 4 * 512])
    aoT = dout("aoT", [64, SQ], BF16)
    rout = dout("rout", [SQ, 128], BF16)
    cout = dout("cout", [SQ, 64], BF16)

    with contextlib.ExitStack() as st:
        S = Sched(nc, st)
        wbf = S.sb("wbf", [128, 16 * M_NCOL], BF16)
        xt = S.sb("xt", [128, 16 * 512], BF16)
        QT = S.sb("QT", [64, SQ], BF16)
        KT = S.sb("KT", [64, SQ], BF16)
        Vsb = S.sb("Vsb", [128, NCH * 64], BF16)
        cs = S.sb("cs", [128, 512], F32)
        sn = S.sb("sn", [128, 512], F32)
        H = S.sb("H", [128, 4 * 512], F32)
        t1 = S.sb("t1", [128, 512], F32)
        t2 = S.sb("t2", [128, 512], F32)
        qfT = S.sb("qfT", [128, 512], BF16)
        kfT = S.sb("kfT", [128, 512], BF16)
        C = S.sb("C", [128, C_W], F32)
        swf = S.sb("swf", [128, 128], F32)
        WTb = S.sb("WTb", [128, 128], BF16)
        MIb = S.sb("MIb", [128, 128], BF16)
        ONb = S.sb("ONb", [128, 128], BF16)
        IDb = S.sb("IDb", [128, 128], BF16)
        msk = S.sb("msk", [128, 4 * 512], BF16)
        rvb = S.sb("rvb", [128, 4 * 128], BF16)
        sg = S.sb("sg", [128, 4 * 128], F32)
        ug = S.sb("ug", [128, 4 * 192], F32)
        scT = S.sb("scT", [128, 128], BF16)
        kfz = S.sb("kfz", [128, 128], BF16)
        qxT = S.sb("qxT", [128, 128], BF16)
        state = S.sb("state", [128, 128], F32)
        prevb = S.sb("prevb", [128, 128], BF16)
        st6 = S.sb("st6", [128, 6], F32)
        mv = S.sb("mv", [128, 4], F32)
        yt = S.sb("yt", [128, 128], F32)
        rst = S.sb("rst", [128, 4 * 128], BF16)
        st6b = S.sb("st6b", [128, 6], F32)
        mvb = S.sb("mvb", [128, 4], F32)
        vn = S.sb("vn", [128, 64], F32)
        vn2 = S.sb("vn2", [128, 64], F32)
        vg = S.sb("vg", [128, 64], BF16)
        cstg = S.sb("cstg", [128, 4 * 64], BF16)
        ez = [S.sb("ez%d" % i, [128, 512], F32) for i in range(2)]
        sp = [S.sb("sp%d" % i, [128, 512], BF16) for i in range(2)]
        eF = [S.sb("eF%d" % i, [128, 512], F32) for i in range(2)]
        aa = [S.sb("aa%d" % i, [128, 512], BF16) for i in range(2)]
        Srun = [S.sb("Srun%d" % i, [128, 512], BF16) for i in range(2)]
        aost = S.sb("aost", [64, 512], BF16)
        bk = [S.ps("bk%d" % i, [128, 512], F32) for i in range(7)]
        bkT = S.ps("bkT", [128, 1024], BF16)
        o_aoT, o_rout, o_cout = S.buf("o_aoT"), S.buf("o_rout"), S.buf("o_cout")

        S.dma("pool", wbf[:], w, [], [wbf], "wbf")
        S.dma("sp", C[:], cst, [], [C], "C")
        S.dma("sp", swf[:], swT, [], [swf], "swf")
        S.dma("pool", msk[:], sbmask, [], [msk], "msk")
        S.tt("dve", WTb[:], swf[:], C[:, C_WM:C_WM + 128], ALU.mult, [swf, C], [WTb])
        S.cp("dve", MIb[:], C[:, C_MI:C_MI + 128], [C], [MIb])
        S.cp("dve", IDb[:], C[:, C_ID:C_ID + 128], [C], [IDb])
        S.op("pool", lambda e: e.memset(ONb[:], 1.0), [], [ONb])
        S.op("pool", lambda e: e.memset(state[:], 0.0), [], [state])
        S.op("pool", lambda e: e.memset(prevb[:], 0.0), [], [prevb])
        DT = C[:, C_DT:C_DT + 128]
        XI = C[:, C_XI:C_XI + 128]

        fm = [("q", 0, 64), ("k", 64, 64), ("rq", 128, 128), ("rqp", 256, 128), ("rk", 384, 128), ("rkp", 512, 128)]
        for n in range(NT):
            tsl = slice(n * 512, (n + 1) * 512)
            S.dma("pool", xt[:], xT[n], [], [xt], "xt")
            S.dma("sp", cs[:], cosT[:, tsl], [], [cs], "cs")
            S.dma("sp", sn[:], sinT[:, tsl], [], [sn], "sn")
            for gi, (nm, c0, cw) in enumerate(fm):
                b = bk[gi % 2]
                for dc in range(16):
                    S.mm(b[0:cw, :], wbf[:, dc * M_NCOL + c0: dc * M_NCOL + c0 + cw],
                         xt[:, dc * 512:(dc + 1) * 512], dc == 0, dc == 15, [wbf, xt], [b])
                if nm == "q":
                    S.act(QT[0:64, tsl], b[0:64, :], AF.Copy, [b], [QT], scale=0.125)
                elif nm == "k":
                    S.act(KT[0:64, tsl], b[0:64, :], AF.Copy, [b], [KT])
                elif nm in ("rq", "rk"):
                    S.tt("dve", t1[:], b[:, :], cs[:], ALU.mult, [b, cs], [t1])
                else:
                    S.tt("dve", t2[:], b[:, :], sn[:], ALU.mult, [b, sn], [t2])
                    dst = qfT if nm == "rqp" else kfT
                    S.tt("pool", dst[:], t1[:], t2[:], ALU.add, [t1, t2], [dst])
            for j in range(4):
                b = bk[2]
                for dc in range(16):
                    S.mm(b[:, :], xt[:, dc * 512 + j * 128: dc * 512 + (j + 1) * 128],
                         wbf[:, dc * M_NCOL + M_FM: dc * M_NCOL + M_NCOL], dc == 0, dc == 15, [xt, wbf], [b])
                S.cp("dve", H[:, j * 512:(j + 1) * 512], b[:, :], [b], [H])
            Hv = H[:].rearrange("p (j c) -> p j c", j=4)
            S.cp("pool", Vsb[:, n * 256:(n + 1) * 256].rearrange("p (j c) -> p j c", j=4), Hv[:, :, 0:64], [H], [Vsb])
            S.cp("pool", rvb[:].rearrange("p (j c) -> p j c", j=4), Hv[:, :, 64:192], [H], [rvb])
            S.act(sg[:].rearrange("p (j c) -> p j c", j=4), Hv[:, :, 192:320], AF.Silu, [H], [sg])
            S.act(ug[:].rearrange("p (j c) -> p j c", j=4), Hv[:, :, 320:512], AF.Gelu_apprx_tanh, [H], [ug])
            for j in range(4):
                cj = slice(j * 128, (j + 1) * 128)
                rv_j = rvb[:, j * 128:(j + 1) * 128]
                S.mm(bk[3][:, 0:128], kfT[:, cj], qfT[:, cj], True, True, [kfT, qfT], [bk[3]])
                S.tt("dve", scT[:], bk[3][:, 0:128], DT, ALU.mult, [bk[3], C], [scT])
                S.tr(bkT[:, 0:128], kfT[:, cj], IDb[:], [kfT, IDb], [bkT])
                S.act(kfz[:], bkT[:, 0:128], AF.Copy, [bkT, C], [kfz], scale=C[:, C_ZETA:C_ZETA + 1])
                S.tt("pool", qxT[:], qfT[:, cj], XI, ALU.mult, [qfT, C], [qxT])
                S.mm(bk[4][:, 0:128], scT[:], rv_j, True, False, [scT, rvb], [bk[4]])
                S.mm(bk[4][:, 0:128], qxT[:], prevb[:], False, True, [qxT, prevb], [bk[4]])
                S.mm(bk[5][:, 0:128], kfz[:], rv_j, True, True, [kfz, rvb], [bk[5]])
                S.stt(state[:], state[:], C[:, C_CD:C_CD + 1], bk[5][:, 0:128], ALU.mult, ALU.add, [state, C, bk[5]], [state])
                S.cp("pool", prevb[:], state[:], [state], [prevb])
                S.op("dve", lambda e: e.bn_stats(out=st6[:], in_=bk[4][:, 0:128]), [bk[4]], [st6])
                S.op("dve", lambda e: e.bn_aggr(out=mv[:, 0:2], in_=st6[:]), [st6], [mv])
                S.ts("dve", mv[:, 2:3], mv[:, 1:2], LN_EPS, ALU.add, [mv], [mv])
                S.tt("pool", mv[:, 3:4], mv[:, 2:3], C[:, C_MH:C_MH + 1], ALU.pow, [mv, C], [mv])
                S.stt(yt[:], bk[4][:, 0:128], mv[:, 0:1], sg[:, cj], ALU.subtract, ALU.mult, [bk[4], mv, sg], [yt])
                S.ts("dve", rst[:, cj], yt[:], mv[:, 3:4], ALU.mult, [yt, mv], [rst])
                v_j = ug[:, j * 192 + 64: j * 192 + 192]
                S.op("dve", lambda e, v_j=v_j: e.bn_stats(out=st6b[:], in_=v_j), [ug], [st6b])
                S.op("dve", lambda e: e.bn_aggr(out=mvb[:, 0:2], in_=st6b[:]), [st6b], [mvb])
                S.ts("dve", mvb[:, 2:3], mvb[:, 1:2], LN_EPS, ALU.add, [mvb], [mvb])
                S.tt("pool", mvb[:, 3:4], mvb[:, 2:3], C[:, C_MH:C_MH + 1], ALU.pow, [mvb, C], [mvb])
                S.ts("dve", vn[:], ug[:, j * 192 + 64: j * 192 + 128], mvb[:, 0:1], ALU.subtract, [ug, mvb], [vn],
                     s2=mvb[:, 3:4], op1=ALU.mult)
                S.tt("pool", vn2[:], vn[:], C[:, C_LNG:C_LNG + 64], ALU.mult, [vn, C], [vn2])
                S.tt("pool", vg[:], vn2[:], C[:, C_LNB:C_LNB + 64], ALU.add, [vn2, C], [vg])
                S.mm(bk[6][:, 0:64], WTb[:], vg[:], True, True, [WTb, vg], [bk[6]])
                S.stt(cstg[:, j * 64:(j + 1) * 64], bk[6][:, 0:64], C[:, C_BS:C_BS + 1], ug[:, j * 192: j * 192 + 64],
                      ALU.add, ALU.mult, [bk[6], C, ug], [cstg])
            S.dma("sp", rout[tsl, :].rearrange("(j p) e -> p j e", p=128), rst[:].rearrange("p (j e) -> p j e", j=4),
                  [rst], [o_rout], "o_rout")
            S.dma("sp", cout[tsl, :].rearrange("(j p) e -> p j e", p=128), cstg[:].rearrange("p (j e) -> p j e", j=4),
                  [cstg], [o_cout], "o_cout")

        cnt = 0
        cur, nxt = 0, 1
        for I in range(NT):
            bo = bk[4 + (I % 2)]
            qsl = slice(I * 512, (I + 1) * 512)
            nk = 4 * I + 4
            for idx, j in enumerate(range(nk - 1, -1, -1)):
                first = idx == 0
                last = j == 0
                k2 = cnt % 2
                cnt += 1
                bz, bF = bk[k2], bk[2 + k2]
                S.mm(bz[:, :], KT[0:64, j * 128:(j + 1) * 128], QT[0:64, qsl], True, True, [KT, QT], [bz])
                S.act(ez[k2][:], bz[:, :], AF.Exp, [bz], [ez[k2]])
                S.act(sp[k2][:], ez[k2][:], AF.Ln, [ez[k2]], [sp[k2]], bias=1.0)
                if j >= 4 * I:
                    r = j - 4 * I
                    mr = msk[:, r * 512:(r + 1) * 512]
                    S.tt("pool", sp[k2][:], sp[k2][:], mr, ALU.mult, [sp[k2], msk], [sp[k2]])
                    S.tt("pool", ez[k2][:], ez[k2][:], mr, ALU.mult, [ez[k2], msk], [ez[k2]])
                S.mm(bF[:, :], MIb[:], sp[k2][:], True, first, [MIb, sp[k2]], [bF])
                if not first:
                    S.mm(bF[:, :], ONb[:], Srun[cur][:], False, True, [ONb, Srun[cur]], [bF])
                    S.tt("dve", Srun[nxt][:], Srun[cur][:], sp[k2][:], ALU.add, [Srun[cur], sp[k2]], [Srun[nxt]])
                else:
                    S.cp("dve", Srun[nxt][:], sp[k2][:], [sp[k2]], [Srun[nxt]])
                cur, nxt = nxt, cur
                S.act(eF[k2][:], bF[:, :], AF.Exp, [bF], [eF[k2]], scale=-1.0)
                S.tt("pool", aa[k2][:], ez[k2][:], eF[k2][:], ALU.mult, [ez[k2], eF[k2]], [aa[k2]])
                S.mm(bo[0:64, :], Vsb[:, j * 64:(j + 1) * 64], aa[k2][:], first, last, [Vsb, aa[k2]], [bo])
            S.act(aost[:], bo[0:64, :], AF.Copy, [bo], [aost])
            S.dma("sp", aoT[:, qsl], aost[:], [aost], [o_aoT], "o_aoT")
        S.finish([o_aoT, o_rout, o_cout])
        S.emit()
    return nc


def m_consts(core, SQ=SEQ):
    h = core
    f32 = np.float32
    half = 64
    inv = (f32(10000.0) ** (-(np.arange(half, dtype=f32)) / f32(half))).astype(f32)
    pos = np.arange(SQ, dtype=f32)
    ang = (pos[:, None] * inv[None, :]).astype(f32).astype(np.float64)
    cos = np.cos(ang).astype(f32).T
    sin = np.sin(ang).astype(f32).T
    cosT = np.ascontiguousarray(np.concatenate([cos, cos], 0))
    sinT = np.ascontiguousarray(np.concatenate([-sin, sin], 0))
    log_gamma = np.log1p(-np.exp2(-5.0 - np.float64(h)))
    idx = np.arange(128, dtype=np.float64)
    cst = np.zeros((128, C_W), f32)
    diff = idx[None, :] - idx[:, None]
    cst[:, C_DT:C_DT + 128] = np.where(diff >= 0, np.exp(log_gamma * np.maximum(diff, 0)), 0.0) * (128.0 ** -0.5)
    cst[:, C_XI:C_XI + 128] = np.exp(log_gamma * (idx + 1.0))[None, :] * (128.0 ** -0.5)
    cst[:, C_WM:C_WM + 128] = (idx[:, None] <= idx[None, :])
    cst[:, C_MI:C_MI + 128] = (idx[:, None] >= idx[None, :])
    cst[:, C_ID:C_ID + 128] = np.eye(128)
    cst[:, C_ZETA] = np.exp(log_gamma * (127.0 - idx))
    cst[:, C_CD] = np.exp(log_gamma * 128.0)
    cst[:, C_MH] = -0.5
    sbmask = np.zeros((128, 4, 512), f32)
    s_ = np.arange(128)[:, None]
    t_ = np.arange(512)[None, :]
    for r in range(4):
        sbmask[:, r, :] = (128 * r + s_) < t_
    return cosT, sinT, cst, sbmask.reshape(128, 2048)


def m_inputs(core, xT_t, w_in_l, sgu_ln_g, sgu_ln_b, sgu_w, sgu_b, consts):
    c = core
    g, hf = c // 2, c % 2
    cosT, sinT, cst, sbmask = consts
    cst = cst.copy()
    sw = lambda a: np.concatenate([a[:, 64:], a[:, :64]], axis=1)
    q = w_in_l[:, c * 64:(c + 1) * 64]
    k = w_in_l[:, 512 + c * 64: 512 + (c + 1) * 64]
    v = w_in_l[:, 1024 + c * 64: 1024 + (c + 1) * 64]
    rq = w_in_l[:, 1536 + c * 128: 1536 + (c + 1) * 128]
    rk = w_in_l[:, 2560 + c * 128: 2560 + (c + 1) * 128]
    rv = w_in_l[:, 3584 + c * 128: 3584 + (c + 1) * 128]
    rg = w_in_l[:, 4608 + c * 128: 4608 + (c + 1) * 128]
    cu = w_in_l[:, 5632 + g * 128 + hf * 64: 5632 + g * 128 + hf * 64 + 64]
    cvf = w_in_l[:, 6144 + g * 128: 6144 + (g + 1) * 128]
    cv = np.concatenate([cvf[:, hf * 64: hf * 64 + 64], cvf[:, (1 - hf) * 64:(1 - hf) * 64 + 64]], axis=1)
    wc = np.concatenate([q, k, rq, sw(rq), rk, sw(rk), v, rv, rg, cu, cv], axis=1)
    wt = np.ascontiguousarray(wc.reshape(16, 128, M_NCOL).transpose(1, 0, 2)).reshape(128, 16 * M_NCOL)
    cst[:, C_BS] = sgu_b[g]
    cst[:, C_LNG:C_LNG + 64] = sgu_ln_g[g * 128 + hf * 64: g * 128 + hf * 64 + 64][None, :]
    cst[:, C_LNB:C_LNB + 64] = sgu_ln_b[g * 128 + hf * 64: g * 128 + hf * 64 + 64][None, :]
    swT = np.ascontiguousarray(sgu_w[g].T)
    return {"xT": xT_t, "w": wt, "cosT": cosT, "sinT": sinT, "cst": cst, "swT": swT, "sbmask": sbmask}


def x_to_tiles(x2d):
    S_ = x2d.shape[0]
    a = x2d.reshape(S_ // 512, 512, 16, 128).transpose(0, 3, 2, 1)
    return np.ascontiguousarray(a).reshape(S_ // 512, 128, 16 * 512)


P_DELTA = 1e-5


def build_P(NST=4, NSC=32, LIMIT=None):
    nc = bass.Bass("TRN2", target_bir_lowering=False)
    TOK = NST * 512
    din = lambda n, s, d=F32: nc.dram_tensor(n, s, d, kind="ExternalInput").ap()
    mixT = din("mixT", [NST * 4, 128, 2048], BF16)
    xres = din("xres", [TOK, 2048])
    wout = din("wout", [16, 128, 2048])
    wq = din("wq", [16, 128, 2048])
    skT = din("skT", [128, 256])
    UT = din("UT", [32, 4, 128, 2048])
    Vt = din("Vt", [128, 128, 2048])
    lnp = din("lnp", [4, 128, 2048])
    idn = din("idn", [128, 128])
    y = nc.dram_tensor("y", [TOK, 2048], F32, kind="ExternalOutput").ap()

    with contextlib.ExitStack() as st:
        S = Sched(nc, st)
        S.limit = LIMIT
        oacc = S.sb("oacc", [128, 4 * 2048], F32)
        x1T = S.sb("x1T", [128, 16 * 512], BF16)
        SA = S.sb("SA", [128, 4 * 8 * 128], F32)
        SB = S.sb("SB", [128, 4 * 8 * 128], F32)
        lnk = S.sb("lnk", [128, 32], F32)
        NSTG = 3
        stg = [S.sb("stg%d" % i, [128, 2048], F32) for i in range(NSTG)]
        Ubf = [S.sb("Ubf%d" % i, [128, 2048], BF16) for i in range(4)]
        Vbf = [S.sb("Vbf%d" % i, [128, 2048], BF16) for i in range(4)]
        scr = S.sb("scr", [128, 4096], F32)
        e2 = S.sb("e2", [128, 4096], BF16)
        g = S.sb("g", [128, 4096], BF16)
        Gt = S.sb("Gt", [128, 512], F32)
        gl = S.sb("gl", [128, 512], F32)
        coef = S.sb("coef", [128, 512], BF16)
        coefT = [S.sb("coefT%d" % i, [128, 512], BF16) for i in range(2)]
        mxt = S.sb("mxt", [128, 2048], BF16)
        xr = S.sb("xr", [128, 2048], F32)
        x1b = S.sb("x1b", [128, 2048], BF16)
        qb = S.sb("qb", [128, 2048], BF16)
        qT = S.sb("qT", [128, 2048], BF16)
        skf = S.sb("skf", [128, 256], F32)
        skb = S.sb("skb", [128, 256], BF16)
        idf = S.sb("idf", [128, 128], F32)
        IDb = S.sb("IDb", [128, 128], BF16)
        mh = S.sb("mh", [128, 1], F32)
        st24 = S.sb("st24", [128, 24], F32)
        mv = S.sb("mv", [128, 4], F32)
        T16 = S.sb("T16", [128, 256], F32)
        tmp = S.sb("tmp", [128, 256], F32)
        cand = Tl(g.t[:].bitcast(F32), g.b)
        B16 = S.sb("B16", [128, 128], F32)
        sm = S.sb("sm", [128, 64], F32)
        bkO = [S.ps("bkO%d" % i, [128, 512], F32) for i in range(4)]
        bkH = [S.ps("bkH%d" % i, [128, 512], F32) for i in range(2)]
        bkT = S.ps("bkT", [128, 1024], BF16)
        o_y = S.buf("o_y")

        S.dma("sp", skf[:], skT, [], [skf], "skf")
        S.dma("sp", idf[:], idn, [], [idf], "idf")
        S.cp("dve", skb[:], skf[:], [skf], [skb])
        S.cp("dve", IDb[:], idf[:], [idf], [IDb])
        S.op("pool", lambda e: e.memset(mh[:], -0.5), [], [mh])

        wcnt = [0]

        def wload(src, dst):
            s = stg[wcnt[0] % NSTG]
            tag = "stg%d" % (wcnt[0] % NSTG)
            wcnt[0] += 1
            S.dma("sp", s[:], src, [], [s], tag)
            S.act(dst[:], s[:], AF.Copy, [s], [dst])

        def layernorm(src, gam, bet, dst_f32):
            (sa_, st_), (da_, dt_) = src, dst_f32
            for q in range(4):
                S.op("dve", lambda e, q=q: e.bn_stats(out=st24[:, q * 6:(q + 1) * 6], in_=sa_[:, q * 512:(q + 1) * 512]), [st_], [st24])
            S.op("dve", lambda e: e.bn_aggr(out=mv[:, 0:2], in_=st24[:]), [st24], [mv])
            S.ts("dve", mv[:, 2:3], mv[:, 1:2], LN_EPS, ALU.add, [mv], [mv])
            S.tt("pool", mv[:, 3:4], mv[:, 2:3], mh[:], ALU.pow, [mv, mh], [mv])
            S.ts("dve", da_, sa_, mv[:, 0:1], ALU.subtract, [st_, mv], [dt_], s2=mv[:, 3:4], op1=ALU.mult)
            S.tt("pool", da_, da_, gam, ALU.mult, [dt_, scr], [dt_])
            S.tt("pool", da_, da_, bet, ALU.add, [dt_, scr], [dt_])

        SAv = SA[:].rearrange("p (j h k) -> p j h k", j=4, h=8)
        SBv = SB[:].rearrange("p (j h k) -> p j h k", j=4, h=8)
        T16v = T16[:].rearrange("p (h q i) -> p h q i", h=8, q=2)
        B16v = B16[:].rearrange("p (h i) -> p h i", h=8)

        for ST in range(NST):
            S.dma("sp", scr[:, 0:2048], lnp[0], [], [scr], "scr")
            S.dma("sp", scr[:, 2048:4096], lnp[1], [], [scr], "scr")
            for j in range(4):
                tile = ST * 4 + j
                S.dma("sp", mxt[:], mixT[tile], [], [mxt], "mxt")
                S.dma("sp", xr[:], xres[tile * 128:(tile + 1) * 128, :], [], [xr], "xr")
                for cc in range(16):
                    wb = Ubf[cc % 4]
                    wload(wout[cc], wb)
                    for cg in range(4):
                        S.mm(bkO[cg][:, :], mxt[:, cc * 128:(cc + 1) * 128], wb[:, cg * 512:(cg + 1) * 512],
                             cc == 0, cc == 15, [mxt, wb], [bkO[cg]])
                for cg in range(4):
                    cs_ = slice(cg * 512, (cg + 1) * 512)
                    S.stt(xr[:, cs_], xr[:, cs_], float(DN_ALPHA), bkO[cg][:, :], ALU.mult, ALU.add, [xr, bkO[cg]], [xr])
                layernorm((xr[:, :], xr), scr[:, 0:2048], scr[:, 2048:4096], (xr[:, :], xr))
                S.act(oacc[:, j * 2048:(j + 1) * 2048], xr[:, :], AF.Copy, [xr], [oacc], scale=float(DN_ALPHA))
                S.cp("dve", x1b[:], xr[:], [xr], [x1b])
                for hf in range(2):
                    for d8 in range(8):
                        dc = hf * 8 + d8
                        S.tr(bkT[:, d8 * 128:(d8 + 1) * 128], x1b[:, dc * 128:(dc + 1) * 128], IDb[:], [x1b, IDb], [bkT])
                    S.cp("act" if hf == 0 else "dve",
                         x1T[:].rearrange("p (c t) -> p c t", c=16)[:, hf * 8:(hf + 1) * 8, j * 128:(j + 1) * 128],
                         bkT[:].rearrange("p (c t) -> p c t", c=8), [bkT], [x1T])
                for dc in range(16):
                    wb = Vbf[dc % 4]
                    wload(wq[dc], wb)
                    for cg in range(4):
                        S.mm(bkO[cg][:, :], x1T[:, dc * 512 + j * 128: dc * 512 + (j + 1) * 128], wb[:, cg * 512:(cg + 1) * 512],
                             dc == 0, dc == 15, [x1T, wb], [bkO[cg]])
                for cg in range(4):
                    S.cp("act" if cg % 2 == 0 else "dve", qb[:, cg * 512:(cg + 1) * 512], bkO[cg][:, :], [bkO[cg]], [qb])
                for hf in range(2):
                    for d8 in range(8):
                        dc = hf * 8 + d8
                        S.tr(bkT[:, d8 * 128:(d8 + 1) * 128], qb[:, dc * 128:(dc + 1) * 128], IDb[:], [qb, IDb], [bkT])
                    S.cp("act" if hf == 0 else "dve", qT[:, hf * 1024:(hf + 1) * 1024], bkT[:, :], [bkT], [qT])
                for hp in range(16):
                    p_ = hp % 2
                    S.mm(bkO[hp // 4][:, (hp % 4) * 128:(hp % 4 + 1) * 128], qT[:, hp * 128:(hp + 1) * 128],
                         skb[:, p_ * 128:(p_ + 1) * 128], True, True, [qT, skb], [bkO[hp // 4]])
                for b4 in range(4):
                    bv = bkO[b4][:].rearrange("p (h q k) -> p h q k", h=2, q=2)
                    S.cp("dve", SAv[:, j, 2 * b4:2 * b4 + 2, :], bv[:, :, 0, :], [bkO[b4]], [SA])
                    S.cp("act", SBv[:, j, 2 * b4:2 * b4 + 2, :], bv[:, :, 1, :], [bkO[b4]], [SB])
                for h in range(8):
                    for q_ in range(2):
                        src = (SAv if q_ == 0 else SBv)[:, j, h, :]
                        srcT = SA if q_ == 0 else SB
                        o = (h * 2 + q_) * 16
                        S.op("dve", lambda e, o=o, src=src: e.max(out=T16[:, o:o + 8], in_=src), [srcT], [T16])
                        S.op("dve", lambda e, o=o, src=src: e.match_replace(out=tmp[:, 0:128], in_to_replace=T16[:, o:o + 8],
                                                                            in_values=src, imm_value=-1e30), [srcT, T16], [tmp])
                        S.op("dve", lambda e, o=o: e.max(out=T16[:, o + 8:o + 16], in_=tmp[:, 0:128]), [tmp], [T16])
                S.tt("pool", cand[:].rearrange("p (h i k) -> p h i k", h=8, i=16),
                     T16v[:, :, 0, :].unsqueeze(3).to_broadcast([128, 8, 16, 16]),
                     T16v[:, :, 1, :].unsqueeze(2).to_broadcast([128, 8, 16, 16]), ALU.add, [T16], [cand])
                for h in range(8):
                    ch = cand[:, h * 256:(h + 1) * 256]
                    o = h * 16
                    S.op("dve", lambda e, o=o, ch=ch: e.max(out=B16[:, o:o + 8], in_=ch), [cand], [B16])
                    S.op("dve", lambda e, o=o, ch=ch: e.match_replace(out=tmp[:, :], in_to_replace=B16[:, o:o + 8],
                                                                      in_values=ch, imm_value=-1e30), [cand, B16], [tmp])
                    S.op("dve", lambda e, o=o: e.max(out=B16[:, o + 8:o + 16], in_=tmp[:, :]), [tmp], [B16])
                S.ts("dve", sm[:, 0:8], B16v[:, :, 0], -1.0, ALU.mult, [B16], [sm])
                for h in range(8):
                    S.act(sm[:, 32:48], B16[:, h * 16:(h + 1) * 16], AF.Exp, [B16, sm], [sm],
                          bias=sm[:, h:h + 1], accum=sm[:, 8 + h:9 + h])
                S.act(sm[:, 16:24], sm[:, 8:16], AF.Ln, [sm], [sm])
                S.ts("dve", sm[:, 24:32], B16v[:, :, 15], -P_DELTA, ALU.add, [B16], [sm])
                S.tt("dve", lnk[:, j * 8:(j + 1) * 8], sm[:, 24:32], sm[:, 0:8], ALU.add, [sm], [lnk])
                S.tt("dve", lnk[:, j * 8:(j + 1) * 8], lnk[:, j * 8:(j + 1) * 8], sm[:, 16:24], ALU.subtract, [sm, lnk], [lnk])
                S.tt("dve", SAv[:, j, :, :], SAv[:, j, :, :], sm[:, 24:32].unsqueeze(2).to_broadcast([128, 8, 128]),
                     ALU.subtract, [SA, sm], [SA])

            def hid_and_coef(sc, j):
                ia0 = sc * 4
                S.tt("pool", scr[:].rearrange("p (h a b) -> p h a b", h=8, a=4),
                     SAv[:, j, :, ia0:ia0 + 4].unsqueeze(3).to_broadcast([128, 8, 4, 128]),
                     SBv[:, j, :, :].unsqueeze(2).to_broadcast([128, 8, 4, 128]), ALU.add, [SA, SB], [scr])
                for h in range(8):
                    S.act(e2[:, h * 512:(h + 1) * 512], scr[:, h * 512:(h + 1) * 512], AF.Exp, [scr, lnk], [e2],
                          bias=lnk[:, j * 8 + h:j * 8 + h + 1])
                S.stt(g[:, :], scr[:, :], 0.0, e2[:, :], ALU.is_ge, ALU.mult, [scr, e2], [g])
                S.op("dve", lambda e: e.tensor_reduce(out=Gt[:, :], in_=g[:].rearrange("p (h e) -> p e h", h=8),
                                                      axis=mybir.AxisListType.X, op=ALU.add), [g], [Gt])
                bh = bkH[j % 2]
                for dc in range(16):
                    S.mm(bh[:, :], x1T[:, dc * 512 + j * 128: dc * 512 + (j + 1) * 128],
                         Ubf[dc // 4][:, (dc % 4) * 512:(dc % 4 + 1) * 512], dc == 0, dc == 15, [x1T, Ubf[dc // 4]], [bh])
                S.act(gl[:, :], bh[:, :], AF.Gelu_apprx_tanh, [bh], [gl])
                S.tt("pool", coef[:, :], Gt[:, :], gl[:, :], ALU.mult, [Gt, gl], [coef])
                for q in range(4):
                    S.tr(bkT[:, q * 128:(q + 1) * 128], coef[:, q * 128:(q + 1) * 128], IDb[:], [coef, IDb], [bkT])
                S.cp("act", coefT[j % 2][:, :], bkT[:, 0:512], [bkT], [coefT[j % 2]])

            def out_mm(sc, j):
                for cg in range(4):
                    for q in range(4):
                        S.mm(bkO[cg][:, :], coefT[j % 2][:, q * 128:(q + 1) * 128], Vbf[q][:, cg * 512:(cg + 1) * 512],
                             q == 0, q == 3, [coefT[j % 2], Vbf[q]], [bkO[cg]])
                for cg in range(4):
                    osl = oacc[:, j * 2048 + cg * 512: j * 2048 + (cg + 1) * 512]
                    S.tt("dve", osl, osl, bkO[cg][:, :], ALU.add, [oacc, bkO[cg]], [oacc])

            for sc in range(NSC):
                for q in range(4):
                    wload(UT[sc, q], Ubf[q])
                for q in range(4):
                    wload(Vt[sc * 4 + q], Vbf[q])
                for step in range(5):
                    if step < 4:
                        hid_and_coef(sc, step)
                    if step >= 1:
                        out_mm(sc, step - 1)

            S.dma("sp", scr[:, 0:2048], lnp[2], [], [scr], "scr")
            S.dma("sp", scr[:, 2048:4096], lnp[3], [], [scr], "scr")
            for j in range(4):
                tile = ST * 4 + j
                osl = oacc[:, j * 2048:(j + 1) * 2048]
                layernorm((osl, oacc), scr[:, 0:2048], scr[:, 2048:4096], (osl, oacc))
                S.dma("sp", y[tile * 128:(tile + 1) * 128, :], osl, [oacc], [o_y], "o_y")
        S.limit = None
        print("P ops", S.nops)
        S.finish([o_y])
        S.emit()
    return nc


def p_shared_inputs(w_out_l, wq_l, sub_keys_l, u_l, v_l, ln1g, ln1b, ln2g, ln2b):
    f32 = np.float32
    skT = np.ascontiguousarray(sub_keys_l.transpose(2, 0, 1)).reshape(128, 256)
    UT = np.ascontiguousarray(u_l.reshape(32, 512, 4, 4, 128).transpose(0, 2, 4, 3, 1)).reshape(32, 4, 128, 2048)
    lnp = np.ascontiguousarray(np.broadcast_to(np.stack([ln1g, ln1b, ln2g, ln2b])[:, None, :], (4, 128, 2048))).astype(f32)
    return {"wout": np.ascontiguousarray(w_out_l.reshape(16, 128, 2048)), "wq": np.ascontiguousarray(wq_l.reshape(16, 128, 2048)),
            "skT": skT, "UT": UT, "Vt": np.ascontiguousarray(v_l.reshape(128, 128, 2048)), "lnp": lnp,
            "idn": np.eye(128, dtype=f32)}


def mix_to_tiles(mix_bf):
    S_ = mix_bf.shape[0]
    a = mix_bf.reshape(S_ // 128, 128, 16, 128).transpose(0, 3, 2, 1)
    return np.ascontiguousarray(a).reshape(S_ // 128, 128, 2048)


def assemble_mix(resM, SQ):
    mix = np.zeros((SQ, 2048), dtype=NPBF)
    for c in range(NCORE):
        r = resM[c]
        g_, hf = c // 2, c % 2
        mix[:, c * 64:(c + 1) * 64] = np.asarray(r["aoT"]).T
        mix[:, 512 + c * 128: 512 + (c + 1) * 128] = np.asarray(r["rout"])
        o = 1536 + g_ * 128 + hf * 64
        mix[:, o:o + 64] = np.asarray(r["cout"])
    return mix


_CACHE = {}


def kernel(x, w_in, w_out, sgu_ln_g, sgu_ln_b, sgu_w, sgu_b, ln1_g, ln1_b,
           peer_wq, peer_sub_keys, peer_u, peer_v, ln2_g, ln2_b):
    x = np.asarray(x, dtype=np.float32)
    cur = np.ascontiguousarray(x[0])
    if "M" not in _CACHE:
        _CACHE["M"] = build_M(32)
        _CACHE["P"] = build_P(4)
        _CACHE["mc"] = [m_consts(c, SEQ) for c in range(NCORE)]
    ncM, ncP, mc = _CACHE["M"], _CACHE["P"], _CACHE["mc"]
    for l in range(DEPTH):
        xT_t = x_to_tiles(cur)
        mapsM = [m_inputs(c, xT_t, np.asarray(w_in[l]), np.asarray(sgu_ln_g[l]), np.asarray(sgu_ln_b[l]),
                          np.asarray(sgu_w[l]), np.asarray(sgu_b[l]), mc[c]) for c in range(NCORE)]
        resM = run_bass_kernel_spmd(ncM, mapsM, core_ids=list(range(NCORE))).results
        mix = assemble_mix(resM, SEQ)
        mixT = mix_to_tiles(mix)
        shared = p_shared_inputs(np.asarray(w_out[l]), np.asarray(peer_wq[l]), np.asarray(peer_sub_keys[l]),
                                 np.asarray(peer_u[l]), np.asarray(peer_v[l]), np.asarray(ln1_g[l]), np.asarray(ln1_b[l]),
                                 np.asarray(ln2_g[l]), np.asarray(ln2_b[l]))
        mapsP = []
        for c in range(NCORE):
            d = dict(shared)
            d["mixT"] = np.ascontiguousarray(mixT[c * 16:(c + 1) * 16])
            d["xres"] = np.ascontiguousarray(cur[c * 2048:(c + 1) * 2048])
            mapsP.append(d)
        resP = run_bass_kernel_spmd(ncP, mapsP, core_ids=list(range(NCORE))).results
        cur = np.concatenate([np.asarray(resP[c]["y"]) for c in range(NCORE)], axis=0).astype(np.float32)
    return cur[None]
```
